# Optimizing a Trainium2 kernel written in Bass

```python
import math
import jax, jax.numpy as jnp
from jax import lax
import numpy as np

D_MODEL = 1024
BATCH = 32
SEQ = 2048
DEPTH = 1

CHUNK = 64
QBLOCK = 128
D_CONV = D_MODEL
CONV_WIDTH = 31
N_DIFF_HEADS = 8
DIFF_HEAD_DIM = 64
D_ATTN = N_DIFF_HEADS * 2 * DIFF_HEAD_DIM
ROPE_DIM = DIFF_HEAD_DIM // 4
ROPE_THETA = 500000.0
N_BRANCHES = 2
EPS = 1e-6
SPLITS = (2 * D_CONV, D_CONV, D_ATTN, D_ATTN, D_ATTN, D_ATTN, N_BRANCHES * D_MODEL)
D_IN = sum(SPLITS)

kernel_name = "hybrid_conformer_diffattn_gated_block"


def lambda_init_for(layer):
    return 0.8 - 0.6 * math.exp(-0.3 * layer)


def rms_norm(x, g):
    xf = x.astype(jnp.float32)
    y = xf * lax.rsqrt(jnp.mean(xf * xf, axis=-1, keepdims=True) + EPS)
    return (y * g.astype(jnp.float32)).astype(x.dtype)


def layer_norm(x, g, b):
    xf = x.astype(jnp.float32)
    mu = jnp.mean(xf, axis=-1, keepdims=True)
    var = jnp.mean(jnp.square(xf - mu), axis=-1, keepdims=True)
    y = (xf - mu) * lax.rsqrt(var + EPS)
    return (y * g.astype(jnp.float32) + b.astype(jnp.float32)).astype(x.dtype)


def partial_rope(t, cos, sin):
    half = ROPE_DIM // 2
    x1 = t[..., :half]
    x2 = t[..., half:ROPE_DIM]
    rot = jnp.concatenate([x1 * cos - x2 * sin, x2 * cos + x1 * sin], axis=-1)
    return jnp.concatenate([rot, t[..., ROPE_DIM:]], axis=-1)


def conformer_conv_branch(glu_in, gate, conv_w, conv_b, ln_g, ln_b, w_proj):
    a, b = jnp.split(glu_in, 2, axis=-1)
    u = a * jax.nn.sigmoid(b)
    u = lax.conv_general_dilated(
        u, conv_w, window_strides=(1,), padding=[(CONV_WIDTH - 1, 0)],
        dimension_numbers=("NWC", "WIO", "NWC"),
        feature_group_count=D_CONV) + conv_b
    u = jax.nn.silu(layer_norm(u, ln_g, ln_b)) * jax.nn.silu(gate)
    return u @ w_proj


def diff_attention_branch(q, k, v, gate, positions, lq1, lk1, lq2, lk2,
                          g_subln, w_proj, lambda_init):
    B, T = q.shape[0], q.shape[1]
    q = q.reshape(B, T, N_DIFF_HEADS, 2, DIFF_HEAD_DIM)
    k = k.reshape(B, T, N_DIFF_HEADS, 2, DIFF_HEAD_DIM)
    v = v.reshape(B, T, N_DIFF_HEADS, 2 * DIFF_HEAD_DIM)
    inv_freq = ROPE_THETA ** (-jnp.arange(0, ROPE_DIM, 2, dtype=jnp.float32) / ROPE_DIM)
    ang = positions.astype(jnp.float32)[..., None] * inv_freq
    cos = jnp.cos(ang)[:, :, None, None, :].astype(q.dtype)
    sin = jnp.sin(ang)[:, :, None, None, :].astype(q.dtype)
    q = partial_rope(q, cos, sin) * (DIFF_HEAD_DIM ** -0.5)
    k = partial_rope(k, cos, sin)
    lam = (jnp.exp(jnp.sum(lq1.astype(jnp.float32) * lk1.astype(jnp.float32)))
           - jnp.exp(jnp.sum(lq2.astype(jnp.float32) * lk2.astype(jnp.float32)))
           + lambda_init)
    outs = []
    for i in range(T // QBLOCK):
        q0, q1 = i * QBLOCK, (i + 1) * QBLOCK
        qb = q[:, q0:q1]
        kb = k[:, :q1]
        vb = v[:, :q1]
        s = jnp.einsum("bqhmd,bkhmd->bhmqk", qb, kb).astype(jnp.float32)
        q_chunk = jnp.arange(q0, q1) // CHUNK
        k_chunk = jnp.arange(q1) // CHUNK
        mask = k_chunk[None, :] <= q_chunk[:, None]
        p = jax.nn.softmax(jnp.where(mask, s, -jnp.inf), axis=-1)
        w = p[:, :, 0] - lam * p[:, :, 1]
        outs.append(jnp.einsum("bhqk,bkhe->bqhe", w.astype(v.dtype), vb))
    o = jnp.concatenate(outs, axis=1)
    o = rms_norm(o, g_subln) * (1.0 - lambda_init)
    o = o.reshape(B, T, D_ATTN) * jax.nn.silu(gate)
    return o @ w_proj


def setup_inputs(seed: int = 0) -> dict:
    key = jax.random.key(seed)
    ks = jax.random.split(key, 20)
    f32 = jnp.float32
    nrm = lambda k, shape, s: jax.random.normal(k, shape, f32) * s
    x = jax.random.normal(ks[0], (BATCH, SEQ, D_MODEL), f32)
    positions = jnp.broadcast_to(jnp.arange(SEQ, dtype=jnp.int32)[None, :], (BATCH, SEQ))
    return {
        "x": x,
        "positions": positions,
        "g_pre": 1.0 + nrm(ks[1], (DEPTH, D_MODEL), 0.02),
        "w_in": nrm(ks[2], (DEPTH, D_MODEL, D_IN), D_MODEL ** -0.5),
        "conv_w": nrm(ks[3], (DEPTH, CONV_WIDTH, 1, D_CONV), CONV_WIDTH ** -0.5),
        "conv_b": nrm(ks[4], (DEPTH, D_CONV), 0.02),
        "ln_g": 1.0 + nrm(ks[5], (DEPTH, D_CONV), 0.02),
        "ln_b": nrm(ks[6], (DEPTH, D_CONV), 0.02),
        "w_conv_proj": nrm(ks[7], (DEPTH, D_CONV, D_MODEL), D_CONV ** -0.5),
        "lambda_q1": nrm(ks[8], (DEPTH, DIFF_HEAD_DIM), 0.1),
        "lambda_k1": nrm(ks[9], (DEPTH, DIFF_HEAD_DIM), 0.1),
        "lambda_q2": nrm(ks[10], (DEPTH, DIFF_HEAD_DIM), 0.1),
        "lambda_k2": nrm(ks[11], (DEPTH, DIFF_HEAD_DIM), 0.1),
        "g_subln": 1.0 + nrm(ks[12], (DEPTH, 2 * DIFF_HEAD_DIM), 0.02),
        "w_attn_proj": nrm(ks[13], (DEPTH, D_ATTN, D_MODEL), D_ATTN ** -0.5),
        "w_out": nrm(ks[14], (DEPTH, D_MODEL, D_MODEL), D_MODEL ** -0.5),
        "g_post": 1.0 + nrm(ks[15], (DEPTH, D_MODEL), 0.02),
    }


def reference(x, positions, g_pre, w_in, conv_w, conv_b, ln_g, ln_b, w_conv_proj,
              lambda_q1, lambda_k1, lambda_q2, lambda_k2, g_subln, w_attn_proj,
              w_out, g_post):
    offsets = [int(o) for o in np.cumsum(SPLITS)[:-1]]
    for l in range(DEPTH):
        h = rms_norm(x, g_pre[l])
        proj = h @ w_in[l]
        glu_in, gate_a, q, k, v, gate_b, merge = jnp.split(proj, offsets, axis=-1)
        y_a = conformer_conv_branch(glu_in, gate_a, conv_w[l], conv_b[l], ln_g[l],
                                    ln_b[l], w_conv_proj[l])
        y_b = diff_attention_branch(q, k, v, gate_b, positions, lambda_q1[l],
                                    lambda_k1[l], lambda_q2[l], lambda_k2[l],
                                    g_subln[l], w_attn_proj[l], lambda_init_for(l + 1))
        m_a, m_b = jnp.split(merge, 2, axis=-1)
        y = jax.nn.sigmoid(m_a) * y_a + jax.nn.sigmoid(m_b) * y_b
        out = y @ w_out[l]
        x = x + rms_norm(out, g_post[l])
    return x
```

```python
import math
from contextlib import ExitStack

import numpy as np
import concourse.bass as bass
import concourse.mybir as mybir
from concourse.bass_utils import run_bass_kernel_spmd

F32 = mybir.dt.float32
BF16 = mybir.dt.bfloat16
I32 = mybir.dt.int32
AF = mybir.ActivationFunctionType
ALU = mybir.AluOpType
AX = mybir.AxisListType

D = 1024
NG = 24
NB = 3
NS = 32
EPS = 1e-6
LAMBDA_INIT = 0.8 - 0.6 * math.exp(-0.3 * 1)
ROPE_THETA = 500000.0
TWO_PI = 2.0 * math.pi
MAGIC = float(0x5F3759DF)
CENG = ("pe", "act", "dve", "pool")


class Prog:
    def __init__(self):
        self.ops = {e: [] for e in CENG + ("sp",)}
        self.cnt = {e: 0 for e in CENG}
        self.last_w = {}
        self.readers = {}
        self.waited = {e: {} for e in CENG + ("sp",)}
        self.ndma = 0
        self.dma_final = {}

    def op(self, eng, emit, reads=(), writes=(), dma=False):
        raw, oth = {}, {}

        def add(d, tok):
            k, v, e = tok
            if d.get(k, (0, None))[0] < v:
                d[k] = (v, e)

        for k in reads:
            if k in self.last_w:
                add(raw, self.last_w[k])
        for k in writes:
            if k in self.last_w:
                add(oth, self.last_w[k])
            for tok in self.readers.get(k, {}).values():
                add(oth, tok)
        if dma:
            n = self.ndma
            self.ndma += 1
            slot = n % NS
            val = 16 * (n // NS + 1)
            tok = (("d", slot), val, "dma")
            if val > 16:
                add(oth, (("d", slot), val - 16, "dma"))
            self.dma_final[slot] = val
        else:
            self.cnt[eng] += 1
            tok = (eng, self.cnt[eng], eng)
        waits = []
        wd = self.waited[eng]
        for d, is_raw in ((raw, True), (oth, False)):
            for k, (v, e) in d.items():
                if e == eng and not dma:
                    if eng == "pe" or not is_raw:
                        continue
                if wd.get(k, 0) >= v:
                    continue
                wd[k] = v
                waits.append((k, v))
        for k in writes:
            self.last_w[k] = tok
            self.readers[k] = {}
        for k in reads:
            self.readers.setdefault(k, {})
            r = self.readers[k]
            if r.get(tok[0], (None, 0, None))[1] < tok[1]:
                r[tok[0]] = tok
        self.ops[eng].append((waits, emit, tok, dma))
        return tok

    def barrier(self):
        toks = {}
        for e in CENG:
            if self.cnt[e]:
                toks[e] = self.cnt[e]
        for s, v in self.dma_final.items():
            toks[("d", s)] = v
        for e in CENG + ("sp",):
            waits = []
            for k, v in toks.items():
                if k == e:
                    continue
                if self.waited[e].get(k, 0) < v:
                    self.waited[e][k] = v
                    waits.append((k, v))
            if waits:
                self.ops[e].append((waits, None, None, False))

    def emit_engine(self, eng_name, eng, sems, dsems):
        for waits, emit, tok, dma in self.ops[eng_name]:
            for k, v in waits:
                h = dsems[k[1]] if isinstance(k, tuple) else sems[k]
                eng.wait_ge(h, v)
            if emit is None:
                continue
            ins = emit(eng)
            if dma:
                ins.then_inc(dsems[tok[0][1]], 16)
            else:
                ins.then_inc(sems[eng_name], 1)


def build_program(nseq, T):
    NT = T // 512
    NBLK = T // 128
    nc = bass.Bass("TRN2", target_bir_lowering=False)
    x = nc.dram_tensor("x", [nseq, T, D], F32, kind="ExternalInput").ap()
    pos = nc.dram_tensor("pos", [128, nseq * NBLK], I32, kind="ExternalInput").ap()
    wall = nc.dram_tensor("wall", [NG, 128, 4096], F32, kind="ExternalInput").ap()
    pbc = nc.dram_tensor("pbc", [1, 2304], F32, kind="ExternalInput").ap()
    pch = nc.dram_tensor("pch", [128, 273], F32, kind="ExternalInput").ap()
    out = nc.dram_tensor("out", [nseq, T, D], F32, kind="ExternalOutput").ap()
    wsc = nc.dram_tensor("wsc", [NG, 128, 4096], BF16).ap()

    P = Prog()
    with ExitStack() as es:
        def sb(name, shape, dt):
            return es.enter_context(nc.sbuf_tensor(name, shape, dt))

        ident = sb("ident", [128, 128], BF16)
        identf = sb("identf", [128, 128], F32)
        ones = sb("ones", [128, 128], BF16)
        pb = sb("pb", [128, 2304], F32)
        pc = sb("pc", [128, 273], F32)
        wch = sb("wch", [128, 8, 31], BF16)
        gh = sb("gh", [128, 8], F32)
        bh = sb("bh", [128, 8], F32)
        sm = sb("sm", [128, 16], F32)
        posi = sb("posi", [128, nseq * NBLK], I32)
        posf = sb("posf", [128, nseq * NBLK], F32)
        invf = sb("invf", [128, 8], F32)
        cos_t = sb("cos_t", [128, nseq * NBLK, 8], F32)
        sin_t = sb("sin_t", [128, nseq * NBLK, 8], F32)
        kT = sb("kT", [128, 8, T], BF16)
        Va = sb("Va", [128, NBLK, 8, 130], BF16)
        halo = sb("halo", [128, 8, 30], BF16)
        ub = [sb(f"ub{i}", [128, 544], BF16) for i in range(2)]
        wbuf = [sb(f"wbuf{i}", [128, 8, 512], BF16) for i in range(NB)]
        xin = [sb(f"xin{i}", [128, 1024], F32) for i in range(2)]
        xres = [sb(f"xres{i}", [128, 1024], F32) for i in range(1)]
        otm = [sb(f"otm{i}", [128, 1024], F32) for i in range(2)]
        xn = sb("xn", [128, 1024], BF16)
        hT = sb("hT", [128, 8, 512], BF16)
        bA = sb("bA", [128, 8, 512], BF16)
        bB = sb("bB", [128, 8, 512], BF16)
        zbT = sb("zbT", [128, 8, 512], BF16)
        dg = [sb(f"dg{i}", [128, 16, 128], BF16) for i in range(2)]
        vv = sb("vv", [128, 8, 512], F32)
        Fp = [sb(f"F{i}", [128, 512], F32) for i in range(6)]
        Hp = [sb(f"H{i}", [128, 512], BF16) for i in range(6)]
        rt = [sb(f"rt{i}", [128, 8, 2, 8], F32) for i in range(2)]
        st8 = sb("st8", [128, 8], F32)
        st4 = [sb(f"st4_{i}", [128, 4], F32) for i in range(4)]
        stO = sb("stO", [128, 8], F32)
        ps = [es.enter_context(nc.psum_tensor(f"ps{i}", [128, 512], F32)) for i in range(7)]
        pT = es.enter_context(nc.psum_tensor("pT", [128, 1024], BF16))
        sems = {e: es.enter_context(nc.semaphore(f"s_{e}")) for e in CENG}
        dsems = [es.enter_context(nc.semaphore(f"d{i}")) for i in range(NS)]

        P.op("sp", lambda e: e.dma_start(out=pb[:], in_=pbc.partition_broadcast(128)), writes=["pb"], dma=True)
        P.op("sp", lambda e: e.dma_start(out=pc[:], in_=pch), writes=["pc"], dma=True)
        P.op("sp", lambda e: e.dma_start(out=posi[:], in_=pos), writes=["posi"], dma=True)

        def _ident(e):
            e.memset(identf[:], 0.0)
            return e.affine_select(out=identf[:], in_=identf[:], pattern=[[-1, 128]],
                                   compare_op=ALU.not_equal, fill=1.0, base=0, channel_multiplier=1)
        P.op("pool", _ident, writes=["identf"])
        P.op("dve", lambda e: e.tensor_copy(out=ident[:], in_=identf[:]), reads=["identf"], writes=["ident"])
        P.op("pool", lambda e: e.memset(ones[:], 1.0), writes=["ones"])
        P.op("pool", lambda e: e.memset(Va[:, :, :, 128:130], 1.0), writes=["Va"])

        def _invf(e):
            for j in range(8):
                ins = e.memset(invf[:, j:j + 1], float(ROPE_THETA ** (-(2.0 * j) / 16.0)))
            return ins
        P.op("pool", _invf, writes=["invf"])

        def _mask(e):
            e.memset(sm[0:64, 0:1], 0.0)
            return e.memset(sm[64:128, 0:1], -30000.0)
        P.op("pool", _mask, writes=["sm0"])
        maskb = sm[:, 0:1]
        nlam = sm[:, 1:2]
        P.op("dve", lambda e: e.tensor_scalar(out=wch[:], in0=pc[:, 0:248].rearrange("p (c k) -> p c k", k=31),
                                              scalar1=0.5, scalar2=None, op0=ALU.mult), reads=["pc"], writes=["wch"])
        P.op("dve", lambda e: e.tensor_scalar(out=gh[:], in0=pc[:, 256:264], scalar1=0.5, scalar2=None, op0=ALU.mult),
             reads=["pc"], writes=["gh"])
        P.op("dve", lambda e: e.tensor_scalar(out=bh[:], in0=pc[:, 264:272], scalar1=0.5, scalar2=None, op0=ALU.mult),
             reads=["pc"], writes=["bh"])
        cb = pc[:, 248:256]
        gsub = pc[:, 272:273]
        gpre = pb[:, 0:1024]
        gpost = pb[:, 1024:2048]
        P.op("dve", lambda e: e.scalar_tensor_tensor(out=Fp[0][:, 0:64], in0=pb[:, 2048:2112], scalar=1.0,
                                                     in1=pb[:, 2112:2176], op0=ALU.mult, op1=ALU.mult,
                                                     accum_out=sm[:, 2:3]), reads=["pb"], writes=["F0", "sm2"])
        P.op("dve", lambda e: e.scalar_tensor_tensor(out=Fp[0][:, 64:128], in0=pb[:, 2176:2240], scalar=1.0,
                                                     in1=pb[:, 2240:2304], op0=ALU.mult, op1=ALU.mult,
                                                     accum_out=sm[:, 3:4]), reads=["pb"], writes=["F0", "sm3"])
        P.op("act", lambda e: e.activation(out=sm[:, 4:6], in_=sm[:, 2:4], func=AF.Exp), reads=["sm2", "sm3"], writes=["sm4"])
        P.op("dve", lambda e: e.tensor_tensor(out=sm[:, 6:7], in0=sm[:, 5:6], in1=sm[:, 4:5], op=ALU.subtract),
             reads=["sm4"], writes=["sm6"])
        P.op("dve", lambda e: e.tensor_scalar(out=sm[:, 1:2], in0=sm[:, 6:7], scalar1=-LAMBDA_INIT, scalar2=None, op0=ALU.add),
             reads=["sm6"], writes=["sm1"])

        NPB = nseq * NBLK
        P.op("dve", lambda e: e.tensor_copy(out=posf[:], in_=posi[:]), reads=["posi"], writes=["posf"])
        assert NPB * 8 <= 512
        angv = Fp[1][:, 0:NPB * 8].rearrange("p (b j) -> p b j", j=8)
        P.op("dve", lambda e: e.tensor_tensor(out=angv, in0=posf[:, :, None].broadcast_to([128, NPB, 8]),
                                              in1=invf[:, None, :].broadcast_to([128, NPB, 8]), op=ALU.mult),
             reads=["posf", "invf"], writes=["F1"])

        def range_reduce_sin(dst, shift, tag):
            a = Fp[2][:, 0:NPB * 8]
            kf = Fp[3][:, 0:NPB * 8]
            ki = Fp[4][:, 0:NPB * 8].bitcast(I32)
            m = Fp[5][:, 0:NPB * 8]
            src = Fp[1][:, 0:NPB * 8]
            P.op("dve", lambda e: e.tensor_scalar(out=a, in0=src, scalar1=shift, scalar2=None, op0=ALU.add),
                 reads=["F1"], writes=["F2"])
            P.op("dve", lambda e: e.tensor_scalar(out=kf, in0=a, scalar1=1.0 / TWO_PI, scalar2=None, op0=ALU.mult),
                 reads=["F2"], writes=["F3"])
            P.op("dve", lambda e: e.tensor_copy(out=ki, in_=kf), reads=["F3"], writes=["F4"])
            P.op("dve", lambda e: e.tensor_copy(out=kf, in_=ki), reads=["F4"], writes=["F3"])
            hi = 6.28125
            lo = TWO_PI - hi
            P.op("dve", lambda e: e.scalar_tensor_tensor(out=m, in0=kf, scalar=-hi, in1=a, op0=ALU.mult, op1=ALU.add),
                 reads=["F3", "F2"], writes=["F5"])
            P.op("dve", lambda e: e.scalar_tensor_tensor(out=m, in0=kf, scalar=-lo, in1=m, op0=ALU.mult, op1=ALU.add),
                 reads=["F3", "F5"], writes=["F5"])
            P.op("dve", lambda e: e.tensor_scalar(out=kf, in0=m, scalar1=math.pi, scalar2=-TWO_PI, op0=ALU.is_gt, op1=ALU.mult),
                 reads=["F5"], writes=["F3"])
            P.op("dve", lambda e: e.tensor_tensor(out=m, in0=m, in1=kf, op=ALU.add), reads=["F3", "F5"], writes=["F5"])
            P.op("dve", lambda e: e.tensor_scalar(out=kf, in0=m, scalar1=-math.pi, scalar2=TWO_PI, op0=ALU.is_lt, op1=ALU.mult),
                 reads=["F5"], writes=["F3"])
            P.op("dve", lambda e: e.tensor_tensor(out=m, in0=m, in1=kf, op=ALU.add), reads=["F3", "F5"], writes=["F5"])
            P.op("dve", lambda e: e.tensor_scalar(out=m, in0=m, scalar1=3.14159, scalar2=-3.14159, op0=ALU.min, op1=ALU.max),
                 reads=["F5"], writes=["F5"])
            P.op("act", lambda e: e.activation(out=dst[:].rearrange("p b j -> p (b j)"), in_=m, func=AF.Sin),
                 reads=["F5"], writes=[tag])
        range_reduce_sin(sin_t, 0.0, "sin_t")
        range_reduce_sin(cos_t, math.pi / 2.0, "cos_t")
        P.barrier()

        ORDER = list(range(6, 14)) + list(range(0, 6)) + list(range(14, 24))
        GPOS = {g: i for i, g in enumerate(ORDER)}
        for g in ORDER:
            P.op("pool", lambda e, g=g: e.dma_start(out=wsc[g], in_=wall[g]), writes=[("wsc", g)], dma=True)
        state = {"wn": 0, "bank": 0}
        tiles = [(sq, j) for sq in range(nseq) for j in range(NT)]
        total_w = NG * len(tiles)

        def issue_wload(n):
            if n >= total_w:
                return
            g = ORDER[n % NG]
            slot = n % NB
            P.op("sp", lambda e: e.dma_start(out=wbuf[slot][:].rearrange("p a b -> p (a b)"), in_=wsc[g]),
                 reads=[("wsc", g)], writes=[("w", slot)], dma=True)

        for n in range(NB):
            issue_wload(n)

        def next_bank(lim=7):
            b = state["bank"] % lim
            state["bank"] += 1
            return b

        def rsqrt_newton(eng, xt, yt, tt, keys, iters=3):
            kx, ky, kt = keys
            P.op(eng, lambda e: e.tensor_scalar(out=yt.bitcast(I32), in0=xt.bitcast(I32), scalar1=1, scalar2=None,
                                                op0=ALU.arith_shift_right), reads=[kx], writes=[ky])
            P.op(eng, lambda e: e.tensor_scalar(out=yt.bitcast(I32), in0=yt.bitcast(I32), scalar1=-1.0, scalar2=MAGIC,
                                                op0=ALU.mult, op1=ALU.add), reads=[ky], writes=[ky])
            for _ in range(iters):
                P.op(eng, lambda e: e.tensor_tensor(out=tt, in0=yt, in1=yt, op=ALU.mult), reads=[ky], writes=[kt])
                P.op(eng, lambda e: e.tensor_tensor(out=tt, in0=tt, in1=xt, op=ALU.mult), reads=[kt, kx], writes=[kt])
                P.op(eng, lambda e: e.tensor_scalar(out=tt, in0=tt, scalar1=-0.5, scalar2=1.5, op0=ALU.mult, op1=ALU.add),
                     reads=[kt], writes=[kt])
                P.op(eng, lambda e: e.tensor_tensor(out=yt, in0=yt, in1=tt, op=ALU.mult), reads=[ky, kt], writes=[ky])

        xissued = set()

        def load_x(tidx, s):
            if tidx >= len(tiles) or (tidx, s) in xissued:
                return
            xissued.add((tidx, s))
            sq_, j_ = tiles[tidx]
            xb = (tidx * 4 + s) % 2
            t0_ = j_ * 512 + s * 128
            P.op("sp", lambda e: e.dma_start(out=xin[xb][:], in_=x[sq_, t0_:t0_ + 128, :]), writes=[("xin", xb)], dma=True)

        def do_tile(ti, sq, j):
            tok0 = j * 512
            blk0 = sq * NBLK + j * 4
            if j == 0:
                P.op("pool", lambda e: e.memset(halo[:], 0.0), writes=["halo"])

            for s in range(4):
                xb = (ti * 4 + s) % 2
                load_x(ti, s)
                P.op("act", lambda e, xb=xb: e.activation(out=xn[:], in_=xin[xb][:], func=AF.Square, accum_out=st4[0][:, 0:1]),
                     reads=[("xin", xb)], writes=["xn", "s40"])
                P.op("dve", lambda e: e.tensor_scalar(out=st4[0][:, 1:2], in0=st4[0][:, 0:1], scalar1=1.0 / D, scalar2=EPS,
                                                      op0=ALU.mult, op1=ALU.add), reads=["s40"], writes=["s41"])
                rsqrt_newton("dve", st4[0][:, 1:2], st4[0][:, 2:3], st4[0][:, 3:4], ("s41", "s42", "s43"))
                P.op("dve", lambda e, xb=xb: e.scalar_tensor_tensor(out=xn[:], in0=xin[xb][:], scalar=st4[0][:, 2:3], in1=gpre,
                                                                    op0=ALU.mult, op1=ALU.mult),
                     reads=[("xin", xb), "s42"], writes=["xn"])

                def _tr(e):
                    for dk in range(8):
                        ins = e.transpose(out=pT[:, dk * 128:(dk + 1) * 128], in_=xn[:, dk * 128:(dk + 1) * 128], identity=ident[:])
                    return ins
                P.op("pe", _tr, reads=["xn"], writes=["pT"])
                P.op("act", lambda e, s=s: e.activation(out=hT[:, :, s * 128:(s + 1) * 128],
                                                        in_=pT[:, :].rearrange("p (a b) -> p a b", b=128), func=AF.Copy),
                     reads=[], writes=["pT", "hT"])

            def slot_of(g):
                return (ti * NG + GPOS[g]) % NB

            def done_group(g):
                issue_wload(ti * NG + GPOS[g] + NB)

            for gi, g in enumerate(range(6, 12)):
                slot = slot_of(g)
                kind = gi // 2
                h0 = (gi % 2) * 4
                for s in range(4):
                    bk = next_bank()

                    def _mm(e, bk=bk, s=s, slot=slot):
                        for dk in range(8):
                            ins = e.matmul(ps[bk][:, :], lhsT=hT[:, dk, s * 128:(s + 1) * 128], rhs=wbuf[slot][:, dk, :],
                                           start=(dk == 0), stop=(dk == 7))
                        return ins
                    P.op("pe", _mm, reads=["hT", ("w", slot)], writes=[("ps", bk)])
                    if kind == 2:
                        P.op("act", lambda e, bk=bk, s=s, h0=h0: e.activation(
                            out=Va[:, j * 4 + s, h0:h0 + 4, 0:128], in_=ps[bk][:, :].rearrange("p (a b) -> p a b", b=128),
                            func=AF.Copy), writes=[("ps", bk), "Va"])
                        continue
                    psv = ps[bk][:, :].rearrange("p (a b) -> p a b", b=64)
                    stg = Hp[4][:, :].rearrange("p (a b) -> p a b", b=64)
                    P.op("act", lambda e, psv=psv, stg=stg: e.activation(out=stg[:, :, 16:64], in_=psv[:, :, 16:64], func=AF.Copy),
                         writes=[("ps", bk), "H4"])
                    xr = psv[:, :, 0:16].rearrange("p a (t d) -> p a t d", d=8)
                    cb_ = cos_t[:, blk0 + s, :][:, None, None, :].broadcast_to([128, 8, 2, 8])
                    sb_ = sin_t[:, blk0 + s, :][:, None, None, :].broadcast_to([128, 8, 2, 8])
                    P.op("dve", lambda e, xr=xr, cb_=cb_: e.tensor_tensor(out=rt[0][:], in0=xr, in1=cb_, op=ALU.mult),
                         writes=[("ps", bk), "rt0"])
                    P.op("dve", lambda e, xr=xr, sb_=sb_: e.tensor_tensor(out=rt[1][:], in0=xr, in1=sb_, op=ALU.mult),
                         writes=[("ps", bk), "rt1"])
                    P.op("dve", lambda e, stg=stg: e.tensor_tensor(out=stg[:, :, 0:8], in0=rt[0][:, :, 0, :], in1=rt[1][:, :, 1, :],
                                                                   op=ALU.subtract), reads=["rt0", "rt1"], writes=["H4"])
                    P.op("dve", lambda e, stg=stg: e.tensor_tensor(out=stg[:, :, 8:16], in0=rt[0][:, :, 1, :], in1=rt[1][:, :, 0, :],
                                                                   op=ALU.add), reads=["rt0", "rt1"], writes=["H4"])

                    def _tr(e):
                        for hh in range(4):
                            ins = e.transpose(out=pT[:, hh * 128:(hh + 1) * 128], in_=Hp[4][:, hh * 128:(hh + 1) * 128], identity=ident[:])
                        return ins
                    P.op("pe", _tr, reads=["H4"], writes=["pT"])
                    if kind == 0:
                        dst = bA[:, h0:h0 + 4, s * 128:(s + 1) * 128]
                        wk = [("A", h0 + i) for i in range(4)]
                    else:
                        dst = kT[:, h0:h0 + 4, tok0 + s * 128: tok0 + (s + 1) * 128]
                        wk = ["kT"]
                    P.op("act", lambda e, dst=dst: e.activation(out=dst, in_=pT[:, 0:512].rearrange("p (a b) -> p a b", b=128), func=AF.Copy),
                         writes=["pT"] + wk)
                done_group(g)

            for g in (12, 13):
                slot = slot_of(g)
                for r in range(4):
                    h = (g - 12) * 4 + r
                    bk = next_bank()
                    fi = h % 2

                    def _mm(e, bk=bk, r=r, slot=slot):
                        for dk in range(8):
                            ins = e.matmul(ps[bk][:, :], lhsT=wbuf[slot][:, dk, r * 128:(r + 1) * 128], rhs=hT[:, dk, :],
                                           start=(dk == 0), stop=(dk == 7))
                        return ins
                    P.op("pe", _mm, reads=["hT", ("w", slot)], writes=[("ps", bk)])
                    P.op("act", lambda e, bk=bk, fi=fi: e.activation(out=Fp[fi][:], in_=ps[bk][:, :], func=AF.Tanh, scale=0.5),
                         writes=[("ps", bk), ("F", fi)])
                    P.op("dve", lambda e, bk=bk, fi=fi, h=h: e.scalar_tensor_tensor(out=bB[:, h, :], in0=Fp[fi][:], scalar=1.0, in1=ps[bk][:, :],
                                                                                    op0=ALU.add, op1=ALU.mult),
                         reads=[("F", fi)], writes=[("ps", bk), ("B", h)])
                done_group(g)

            nkb = 4 * j + 4
            SB = ((0, 1), (2, 3))
            PTB = ((0, 1), (2, 3))
            OB = (4, 5, 6)

            def acc_ap(qb, m):
                a = qb * 2 + m
                return ps[OB[a // 3]][:, (a % 3) * 130:(a % 3) * 130 + 129], OB[a // 3], a
            for h in range(8):
                for kb in range(nkb):
                    i = kb - 4 * j
                    c0 = max(i, 0) * 128
                    par = kb % 2

                    def _s(e, kb=kb, c0=c0, par=par, h=h):
                        for m in range(2):
                            ins = e.matmul(ps[SB[m][par]][:, c0:512], lhsT=kT[m * 64:(m + 1) * 64, h, kb * 128:(kb + 1) * 128],
                                           rhs=bA[m * 64:(m + 1) * 64, h, c0:512], start=True, stop=True)
                        return ins
                    P.op("pe", _s, reads=["kT", ("A", h)], writes=[("ps", SB[0][par]), ("ps", SB[1][par])])
                    for m in range(2):
                        bk = SB[m][par]
                        hb = PTB[m][par]

                        def _e(e, bk=bk, hb=hb, c0=c0, i=i):
                            if i >= 0:
                                e.activation(out=Hp[hb][:, c0:c0 + 64], in_=ps[bk][:, c0:c0 + 64], func=AF.Exp, scale=0.125, bias=maskb)
                                return e.activation(out=Hp[hb][:, c0 + 64:512], in_=ps[bk][:, c0 + 64:512], func=AF.Exp, scale=0.125)
                            return e.activation(out=Hp[hb][:, :], in_=ps[bk][:, :], func=AF.Exp, scale=0.125)
                        P.op("act", _e, writes=[("ps", bk), ("H", hb)])

                    def _pv(e, kb=kb, i=i, par=par, h=h):
                        first_in_bank = {(0, 0), (1, 1), (3, 0)}
                        for qb in range(max(i, 0), 4):
                            for m in range(2):
                                ap_, _, _ = acc_ap(qb, m)
                                ins = e.matmul(ap_, lhsT=Hp[PTB[m][par]][:, qb * 128:(qb + 1) * 128], rhs=Va[:, kb, h, 0:129],
                                               start=(kb == 0 and (qb, m) in first_in_bank), stop=(kb == 4 * j + qb),
                                               skip_group_check=True)
                        return ins
                    P.op("pe", _pv, reads=[("H", PTB[0][par]), ("H", PTB[1][par]), "Va"], writes=[("ps", b) for b in OB])

                def _rec(e):
                    e.reciprocal(out=st8[:, 0:3], in_=ps[4][:, 0:390].rearrange("p (a b) -> p a b", b=130)[:, :, 128])
                    e.reciprocal(out=st8[:, 3:6], in_=ps[5][:, 0:390].rearrange("p (a b) -> p a b", b=130)[:, :, 128])
                    return e.reciprocal(out=st8[:, 6:8], in_=ps[6][:, 0:260].rearrange("p (a b) -> p a b", b=130)[:, :, 128])
                P.op("dve", _rec, writes=[("ps", 4), ("ps", 5), ("ps", 6), "st8"])
                st8v = st8[:, :].rearrange("p (a b) -> p a b", b=2)
                P.op("dve", lambda e: e.tensor_scalar(out=st8v[:, :, 1], in0=st8v[:, :, 1], scalar1=nlam, scalar2=None, op0=ALU.mult),
                     reads=["st8"], writes=["st8"])
                o1 = Fp[4][:, :].rearrange("p (a b) -> p a b", b=128)
                o2 = Fp[5][:, :].rearrange("p (a b) -> p a b", b=128)

                def _o1(e):
                    for qb in range(4):
                        a1, _, _ = acc_ap(qb, 0)
                        ins = e.tensor_scalar(out=o1[:, qb, :], in0=a1[:, 0:128], scalar1=st8[:, 2 * qb:2 * qb + 1], scalar2=None, op0=ALU.mult)
                    return ins

                def _o2(e):
                    for qb in range(4):
                        a2, _, _ = acc_ap(qb, 1)
                        ins = e.scalar_tensor_tensor(out=o2[:, qb, :], in0=a2[:, 0:128], scalar=st8[:, 2 * qb + 1:2 * qb + 2], in1=o1[:, qb, :],
                                                     op0=ALU.mult, op1=ALU.add)
                    return ins
                P.op("dve", _o1, reads=["st8"], writes=[("ps", 4), ("ps", 5), ("ps", 6), ("F", 4)])
                P.op("dve", _o2, reads=["st8", ("F", 4)], writes=[("ps", 4), ("ps", 5), ("ps", 6), ("F", 5)])
                P.op("dve", lambda e: e.tensor_tensor(out=Fp[4][:], in0=Fp[5][:], in1=Fp[5][:], op=ALU.mult),
                     reads=[("F", 5)], writes=[("F", 4)])
                P.op("dve", lambda e: e.tensor_reduce(out=st4[1][:, :], in_=o1, axis=AX.X, op=ALU.add),
                     reads=[("F", 4)], writes=["s4a"])
                P.op("dve", lambda e: e.tensor_scalar(out=st4[1][:, :], in0=st4[1][:, :], scalar1=1.0 / 128.0, scalar2=EPS, op0=ALU.mult, op1=ALU.add),
                     reads=["s4a"], writes=["s4a"])
                rsqrt_newton("dve", st4[1][:, :], st4[2][:, :], st4[3][:, :], ("s4a", "s4b", "s4c"))
                P.op("dve", lambda e: e.tensor_scalar(out=st4[2][:, :], in0=st4[2][:, :], scalar1=(1.0 - LAMBDA_INIT), scalar2=None, op0=ALU.mult),
                     reads=["s4b"], writes=["s4b"])
                onv = Hp[5][:, :].rearrange("p (a b) -> p a b", b=128)
                P.op("dve", lambda e: e.tensor_tensor(out=onv, in0=o2, in1=st4[2][:, :, None].broadcast_to([128, 4, 128]), op=ALU.mult),
                     reads=[("F", 5), "s4b"], writes=[("H", 5)])

                def _tr(e):
                    for qb in range(4):
                        ins = e.transpose(out=pT[:, qb * 128:(qb + 1) * 128], in_=Hp[5][:, qb * 128:(qb + 1) * 128], identity=ident[:])
                    return ins
                P.op("pe", _tr, reads=[("H", 5)], writes=["pT"])
                P.op("dve", lambda e, h=h: e.scalar_tensor_tensor(out=zbT[:, h, :], in0=pT[:, 0:512], scalar=gsub, in1=bB[:, h, :],
                                                                  op0=ALU.mult, op1=ALU.mult),
                     reads=[("B", h)], writes=["pT", ("zb", h)])

            def conv_unit(u):
                g = u // 4
                return slot_of(g), u % 4
            for c in range(8):
                r = c % 2
                banks = {}
                for kind in range(3):
                    u = c * 3 + kind
                    slot, uu = conv_unit(u)
                    bk = next_bank(5)
                    banks[kind] = bk

                    def _mm(e, bk=bk, uu=uu, slot=slot):
                        for dk in range(8):
                            ins = e.matmul(ps[bk][:, :], lhsT=wbuf[slot][:, dk, uu * 128:(uu + 1) * 128], rhs=hT[:, dk, :],
                                           start=(dk == 0), stop=(dk == 7))
                        return ins
                    P.op("pe", _mm, reads=["hT", ("w", slot)], writes=[("ps", bk)])
                    if u % 4 == 3:
                        done_group(u // 4)
                    if kind == 0:
                        P.op("act", lambda e, bk=bk: e.activation(out=Fp[0][:], in_=ps[bk][:, :], func=AF.Tanh, scale=0.5),
                             writes=[("ps", bk), ("F", 0)])
                    elif kind == 1:
                        P.op("pool", lambda e, r=r, c=c: e.tensor_copy(out=ub[r][:, 0:30], in_=halo[:, c, :]), reads=["halo"], writes=[("ubh", r)])
                        P.op("dve", lambda e, bk=bk, r=r: e.scalar_tensor_tensor(out=ub[r][:, 30:542], in0=Fp[0][:], scalar=1.0, in1=ps[bk][:, :],
                                                                                 op0=ALU.add, op1=ALU.mult),
                             reads=[("F", 0)], writes=[("ps", bk), ("ub", r)])
                        P.op("pool", lambda e, r=r, c=c: e.tensor_copy(out=halo[:, c, :], in_=ub[r][:, 512:542]), reads=[("ub", r)], writes=["halo"])
                    else:
                        P.op("act", lambda e, bk=bk: e.activation(out=Fp[1][:], in_=ps[bk][:, :], func=AF.Tanh, scale=0.5),
                             writes=[("ps", bk), ("F", 1)])
                        P.op("dve", lambda e, bk=bk, c=c: e.scalar_tensor_tensor(out=bA[:, c, :], in0=Fp[1][:], scalar=1.0, in1=ps[bk][:, :],
                                                                                 op0=ALU.add, op1=ALU.mult),
                             reads=[("F", 1)], writes=[("ps", bk), ("A", c)])
                for hf in range(2):
                    nt = 16 if hf == 0 else 15
                    P.op("pool", lambda e, hf=hf, nt=nt, c=c: e.tensor_tensor(
                        out=dg[hf][:, 0:nt, :], in0=ident[:, None, :].broadcast_to([128, nt, 128]),
                        in1=wch[:, c, hf * 16:hf * 16 + nt][:, :, None].broadcast_to([128, nt, 128]), op=ALU.mult),
                        writes=[("dg", hf)])
                bv = next_bank(5)
                for hf in range(2):
                    nt = 16 if hf == 0 else 15

                    def _cv(e, hf=hf, nt=nt, r=r, bv=bv):
                        for kk in range(nt):
                            k = hf * 16 + kk
                            ins = e.matmul(ps[bv][:, :], lhsT=dg[hf][:, kk, :], rhs=ub[r][:, k:k + 512], start=(k == 0), stop=(k == 30))
                        return ins
                    P.op("pe", _cv, reads=[("dg", hf), ("ub", r), ("ubh", r)], writes=[("ps", bv)])
                P.op("act", lambda e, bv=bv, c=c: e.activation(out=vv[:, c, :], in_=ps[bv][:, :], func=AF.Identity, bias=cb[:, c:c + 1]),
                     writes=[("ps", bv), ("v", c)])
                P.op("act", lambda e, bv=bv, c=c, r=r: e.activation(out=Hp[2 + r][:], in_=ps[bv][:, :], func=AF.Square, bias=cb[:, c:c + 1]),
                     writes=[("ps", bv), ("H", 2 + r)])
                P.op("pool", lambda e, c=c, r=r: e.tensor_copy(out=Hp[r][:], in_=vv[:, c, :]), reads=[("v", c)], writes=[("H", r)])

                def _st(e, c=c, r=r):
                    e.matmul(ps[5][:, :], lhsT=ones[:], rhs=Hp[r][:], start=(c == 0), stop=(c == 7))
                    return e.matmul(ps[6][:, :], lhsT=ones[:], rhs=Hp[2 + r][:], start=(c == 0), stop=(c == 7))
                P.op("pe", _st, reads=[("H", r), ("H", 2 + r)], writes=[("ps", 5), ("ps", 6)])
            P.op("dve", lambda e: e.tensor_scalar(out=Fp[2][:], in0=ps[5][:, :], scalar1=1.0 / D, scalar2=None, op0=ALU.mult),
                 writes=[("ps", 5), ("F", 2)])
            P.op("dve", lambda e: e.tensor_scalar(out=Fp[4][:], in0=ps[6][:, :], scalar1=1.0 / D, scalar2=None, op0=ALU.mult),
                 writes=[("ps", 6), ("F", 4)])
            P.op("dve", lambda e: e.tensor_tensor(out=Fp[5][:], in0=Fp[2][:], in1=Fp[2][:], op=ALU.mult), reads=[("F", 2)], writes=[("F", 5)])
            P.op("dve", lambda e: e.tensor_tensor(out=Fp[4][:], in0=Fp[4][:], in1=Fp[5][:], op=ALU.subtract),
                 reads=[("F", 4), ("F", 5)], writes=[("F", 4)])
            P.op("dve", lambda e: e.tensor_scalar(out=Fp[4][:], in0=Fp[4][:], scalar1=0.0, scalar2=EPS, op0=ALU.max, op1=ALU.add),
                 reads=[("F", 4)], writes=[("F", 4)])
            rsqrt_newton("dve", Fp[4][:], Fp[3][:], Fp[5][:], (("F", 4), ("F", 3), ("F", 5)))
            P.op("dve", lambda e: e.scalar_tensor_tensor(out=Fp[2][:], in0=Fp[2][:], scalar=-1.0, in1=Fp[3][:], op0=ALU.mult, op1=ALU.mult),
                 reads=[("F", 2), ("F", 3)], writes=[("F", 2)])
            for c in range(8):
                fa, fb = (4, 5) if c % 2 == 0 else (0, 1)
                P.op("dve", lambda e, c=c, fa=fa: e.tensor_tensor(out=Fp[fa][:], in0=vv[:, c, :], in1=Fp[3][:], op=ALU.mult),
                     reads=[("v", c), ("F", 3)], writes=[("F", fa)])
                P.op("dve", lambda e, fa=fa: e.tensor_tensor(out=Fp[fa][:], in0=Fp[fa][:], in1=Fp[2][:], op=ALU.add),
                     reads=[("F", fa), ("F", 2)], writes=[("F", fa)])
                P.op("act", lambda e, c=c, fa=fa, fb=fb: e.activation(out=Fp[fb][:], in_=Fp[fa][:], func=AF.Identity,
                                                                      scale=gh[:, c:c + 1], bias=bh[:, c:c + 1]),
                     reads=[("F", fa)], writes=[("F", fb)])
                P.op("act", lambda e, fa=fa, fb=fb: e.activation(out=Fp[fa][:], in_=Fp[fb][:], func=AF.Tanh),
                     reads=[("F", fb)], writes=[("F", fa)])
                P.op("dve", lambda e, fa=fa, fb=fb: e.scalar_tensor_tensor(out=Fp[fb][:], in0=Fp[fa][:], scalar=1.0, in1=Fp[fb][:],
                                                                           op0=ALU.add, op1=ALU.mult),
                     reads=[("F", fa), ("F", fb)], writes=[("F", fb)])
                P.op("dve", lambda e, c=c, fb=fb: e.tensor_tensor(out=bB[:, c, :], in0=Fp[fb][:], in1=bA[:, c, :], op=ALU.mult),
                     reads=[("F", fb), ("A", c)], writes=[("B", c)])

            load_x(ti + 1, 0)
            load_x(ti + 1, 1)
            for cu in range(8):
                g = 14 + cu
                slot = slot_of(g)
                bks = []
                srcs = [(bB, [("B", k) for k in range(8)]), (zbT, [("zb", k) for k in range(8)]), (hT, ["hT"]), (hT, ["hT"])]
                for r4 in range(4):
                    bk = next_bank()
                    bks.append(bk)
                    src, keys = srcs[r4]

                    def _mm(e, bk=bk, r4=r4, slot=slot, src=src):
                        for dk in range(8):
                            ins = e.matmul(ps[bk][:, :], lhsT=wbuf[slot][:, dk, r4 * 128:(r4 + 1) * 128], rhs=src[:, dk, :],
                                           start=(dk == 0), stop=(dk == 7))
                        return ins
                    P.op("pe", _mm, reads=keys + [("w", slot)], writes=[("ps", bk)])
                done_group(g)
                P.op("act", lambda e, bk=bks[2]: e.activation(out=Fp[0][:], in_=ps[bk][:, :], func=AF.Tanh, scale=0.5),
                     writes=[("ps", bks[2]), ("F", 0)])
                P.op("act", lambda e, bk=bks[3]: e.activation(out=Fp[1][:], in_=ps[bk][:, :], func=AF.Tanh, scale=0.5),
                     writes=[("ps", bks[3]), ("F", 1)])
                P.op("dve", lambda e, bk=bks[0]: e.scalar_tensor_tensor(out=Fp[4][:], in0=Fp[0][:], scalar=1.0, in1=ps[bk][:, :],
                                                                        op0=ALU.add, op1=ALU.mult),
                     reads=[("F", 0)], writes=[("ps", bks[0]), ("F", 4)])
                P.op("dve", lambda e, bk=bks[1]: e.scalar_tensor_tensor(out=Fp[5][:], in0=Fp[1][:], scalar=1.0, in1=ps[bk][:, :],
                                                                        op0=ALU.add, op1=ALU.mult),
                     reads=[("F", 1)], writes=[("ps", bks[1]), ("F", 5)])
                P.op("dve", lambda e, cu=cu: e.tensor_tensor(out=bA[:, cu, :], in0=Fp[4][:], in1=Fp[5][:], op=ALU.add),
                     reads=[("F", 4), ("F", 5)], writes=[("A", cu)])

            s22 = slot_of(22)
            s23 = slot_of(23)
            for s in range(4):
                ob = s % 2
                pair = []
                for half, slot in ((0, s22), (1, s23)):
                    bk = next_bank()
                    pair.append(bk)

                    def _mm(e, bk=bk, slot=slot, s=s):
                        for dk in range(8):
                            ins = e.matmul(ps[bk][:, :], lhsT=bA[:, dk, s * 128:(s + 1) * 128], rhs=wbuf[slot][:, dk, :],
                                           start=(dk == 0), stop=(dk == 7))
                        return ins
                    P.op("pe", _mm, reads=[("A", k) for k in range(8)] + [("w", slot)], writes=[("ps", bk)])
                if s == 3:
                    done_group(22)
                    done_group(23)
                P.op("sp", lambda e, s=s: e.dma_start(out=xres[0][:], in_=x[sq, tok0 + s * 128: tok0 + (s + 1) * 128, :]),
                     writes=["xres"], dma=True)
                for half in range(2):
                    P.op("act", lambda e, half=half, bk=pair[half], ob=ob: e.activation(
                        out=otm[ob][:, half * 512:(half + 1) * 512], in_=ps[bk][:, :], func=AF.Square, accum_out=stO[:, half:half + 1]),
                        writes=[("ps", bk), ("otm", ob), ("sO", half)])
                P.op("dve", lambda e: e.tensor_tensor(out=stO[:, 2:3], in0=stO[:, 0:1], in1=stO[:, 1:2], op=ALU.add),
                     reads=[("sO", 0), ("sO", 1)], writes=[("sO", 2)])
                P.op("dve", lambda e: e.tensor_scalar(out=stO[:, 3:4], in0=stO[:, 2:3], scalar1=0.0625 / D, scalar2=EPS,
                                                      op0=ALU.mult, op1=ALU.add), reads=[("sO", 2)], writes=[("sO", 3)])
                rsqrt_newton("dve", stO[:, 3:4], stO[:, 4:5], stO[:, 5:6], (("sO", 3), ("sO", 4), ("sO", 5)))
                P.op("dve", lambda e: e.tensor_scalar(out=stO[:, 4:5], in0=stO[:, 4:5], scalar1=0.25, scalar2=None, op0=ALU.mult),
                     reads=[("sO", 4)], writes=[("sO", 4)])
                for half in range(2):
                    P.op("dve", lambda e, half=half, bk=pair[half], ob=ob: e.scalar_tensor_tensor(
                        out=otm[ob][:, half * 512:(half + 1) * 512], in0=ps[bk][:, :], scalar=stO[:, 4:5],
                        in1=gpost[:, half * 512:(half + 1) * 512], op0=ALU.mult, op1=ALU.mult),
                        reads=[("sO", 4)], writes=[("ps", bk), ("otm", ob)])
                P.op("pool", lambda e, ob=ob: e.tensor_tensor(out=otm[ob][:], in0=otm[ob][:], in1=xres[0][:], op=ALU.add),
                     reads=["xres", ("otm", ob)], writes=[("otm", ob)])
                P.op("sp", lambda e, s=s, ob=ob: e.dma_start(out=out[sq, tok0 + s * 128: tok0 + (s + 1) * 128, :], in_=otm[ob][:]),
                     reads=[("otm", ob)], dma=True)

        for ti_, (sq_, j_) in enumerate(tiles):
            do_tile(ti_, sq_, j_)
        P.barrier()

        with nc.Block() as block:
            @block.sync
            def _(e):
                P.emit_engine("sp", e, sems, dsems)

            @block.tensor
            def _(e):
                P.emit_engine("pe", e, sems, dsems)

            @block.scalar
            def _(e):
                P.emit_engine("act", e, sems, dsems)

            @block.vector
            def _(e):
                P.emit_engine("dve", e, sems, dsems)

            @block.gpsimd
            def _(e):
                P.emit_engine("pool", e, sems, dsems)
    return nc


def _weight_groups(w_in, w_cp, w_ap, w_out):
    A0, B0, GA0, Q0, K0, V0, GB0, MA0, MB0 = 0, 1024, 2048, 3072, 4096, 5120, 6144, 7168, 8192
    cols = []
    kind_base = (B0, A0, GA0)
    units = []
    for c in range(8):
        for kind in range(3):
            units.append((w_in, kind_base[kind] + c * 128))
    for g in range(6):
        cols.append([units[g * 4 + r] for r in range(4)])
    for base in (Q0, K0, V0):
        for hh in range(2):
            cols.append([(w_in, base + hh * 512 + r * 128) for r in range(4)])
    for hh in range(2):
        cols.append([(w_in, GB0 + hh * 512 + r * 128) for r in range(4)])
    for cu in range(8):
        cols.append([(w_cp, cu * 128), (w_ap, cu * 128), (w_in, MA0 + cu * 128), (w_in, MB0 + cu * 128)])
    for hh in range(2):
        cols.append([(w_out, hh * 512 + r * 128) for r in range(4)])
    assert len(cols) == NG
    wall = np.empty((NG, 128, 8, 512), np.float32)
    for g, us in enumerate(cols):
        for r, (mat, c0) in enumerate(us):
            blk = mat[:, c0:c0 + 128]
            wall[g, :, :, r * 128:(r + 1) * 128] = blk.reshape(8, 128, 128).transpose(1, 0, 2)
    return wall.reshape(NG, 128, 4096)


def _host_params(g_pre, conv_w, conv_b, ln_g, ln_b, lq1, lk1, lq2, lk2, g_subln, g_post):
    pbc = np.concatenate([g_pre.reshape(-1), g_post.reshape(-1), lq1.reshape(-1), lk1.reshape(-1),
                          lq2.reshape(-1), lk2.reshape(-1)]).astype(np.float32).reshape(1, 2304)
    cw = conv_w.reshape(31, 1024).T.reshape(8, 128, 31).transpose(1, 0, 2).reshape(128, 248)
    def pch8(v):
        return v.reshape(8, 128).T
    pch = np.concatenate([cw, pch8(conv_b), pch8(ln_g), pch8(ln_b), g_subln.reshape(128, 1)], axis=1).astype(np.float32)
    return pbc, np.ascontiguousarray(pch)


_CACHE = {}


def run(x, positions, g_pre, w_in, conv_w, conv_b, ln_g, ln_b, w_conv_proj, lambda_q1, lambda_k1, lambda_q2,
        lambda_k2, g_subln, w_attn_proj, w_out, g_post, n_cores=8):
    x = np.asarray(x, np.float32)
    B, T, _ = x.shape
    nseq = B // n_cores
    NBLK = T // 128
    wall = _weight_groups(np.asarray(w_in[0], np.float32), np.asarray(w_conv_proj[0], np.float32),
                          np.asarray(w_attn_proj[0], np.float32), np.asarray(w_out[0], np.float32))
    pbc, pch = _host_params(np.asarray(g_pre[0]), np.asarray(conv_w[0]), np.asarray(conv_b[0]), np.asarray(ln_g[0]),
                            np.asarray(ln_b[0]), np.asarray(lambda_q1[0]), np.asarray(lambda_k1[0]),
                            np.asarray(lambda_q2[0]), np.asarray(lambda_k2[0]), np.asarray(g_subln[0]), np.asarray(g_post[0]))
    positions = np.asarray(positions).astype(np.int32)
    key = (nseq, T)
    if key not in _CACHE:
        _CACHE[key] = build_program(nseq, T)
    nc = _CACHE[key]
    in_maps = []
    for c in range(n_cores):
        xs = np.ascontiguousarray(x[c * nseq:(c + 1) * nseq])
        ps_ = positions[c * nseq:(c + 1) * nseq].reshape(nseq * NBLK, 128).T
        in_maps.append({"x": xs, "pos": np.ascontiguousarray(ps_), "wall": wall, "pbc": pbc, "pch": pch})
    res = run_bass_kernel_spmd(nc, in_maps, core_ids=list(range(n_cores)))
    return np.concatenate([np.asarray(r["out"]) for r in res.results], axis=0).astype(np.float32)


def kernel(**inputs):
    return run(**inputs)
```
